# Optimizing a Trainium2 kernel written in Bass

```python
import jax, jax.numpy as jnp
from jax import lax
import numpy as np

D_MODEL = 4096
BATCH = 1
SEQ = 16384
DEPTH = 4

CHUNK = 64
Q_BLOCK = 128
N_MIXERS = 2
MLA_HEADS = 8
MLA_Q_LORA = 1024
MLA_KV_LORA = 512
MLA_NOPE = 128
MLA_ROPE = 64
MLA_QK_DIM = MLA_NOPE + MLA_ROPE
MLA_V = 128
ROPE_THETA = 10000.0
SB_HEADS = 4
SB_HEAD_DIM = 256
D_FF = 4 * D_MODEL
PLE_DIM = 256
EPS = 1e-6

kernel_name = "hybrid_mla_stickbreaking_trunk"


def rmsnorm(x, gain):
    xf = x.astype(jnp.float32)
    y = xf * lax.rsqrt(jnp.mean(xf * xf, axis=-1, keepdims=True) + EPS)
    return (y * gain.astype(jnp.float32)).astype(x.dtype)


def rope_tables(positions, dtype):
    inv_freq = ROPE_THETA ** (-jnp.arange(0, MLA_ROPE, 2, dtype=jnp.float32) / MLA_ROPE)
    ang = positions.astype(jnp.float32)[..., None] * inv_freq
    return jnp.cos(ang)[:, :, None, :].astype(dtype), jnp.sin(ang)[:, :, None, :].astype(dtype)


def rope_tail(x, cos, sin):
    x_nope, x_rope = x[..., :MLA_NOPE], x[..., MLA_NOPE:]
    x1, x2 = jnp.split(x_rope, 2, axis=-1)
    rot = jnp.concatenate([x1 * cos - x2 * sin, x2 * cos + x1 * sin], axis=-1)
    return jnp.concatenate([x_nope, rot], axis=-1)


def chunk_causal_softmax_attention(q, k, v):
    S = q.shape[1]
    outs = []
    for b in range(S // Q_BLOCK):
        end = (b + 1) * Q_BLOCK
        t = b * Q_BLOCK + jnp.arange(Q_BLOCK)
        kpos = jnp.arange(end)
        mask = (kpos[None, :] // CHUNK) <= (t[:, None] // CHUNK)
        s = jnp.einsum('bqhd,bkhd->bhqk', q[:, b * Q_BLOCK:end], k[:, :end],
                       preferred_element_type=jnp.float32)
        s = jnp.where(mask[None, None], s, -jnp.inf)
        e = jnp.exp(s - jnp.max(s, axis=-1, keepdims=True))
        denom = jnp.sum(e, axis=-1)
        o = jnp.einsum('bhqk,bkhd->bqhd', e.astype(v.dtype), v[:, :end],
                       preferred_element_type=jnp.float32)
        o = o / jnp.swapaxes(denom, 1, 2)[..., None]
        outs.append(o.astype(v.dtype))
    return jnp.concatenate(outs, axis=1)


def stick_breaking_attention(q, k, v):
    S = q.shape[1]
    B, _, H, _ = q.shape
    later_in_block = (jnp.arange(Q_BLOCK)[:, None] > jnp.arange(Q_BLOCK)[None, :]).astype(jnp.float32)
    outs = []
    for b in range(S // Q_BLOCK):
        end = (b + 1) * Q_BLOCK
        nk = b + 1
        t = b * Q_BLOCK + jnp.arange(Q_BLOCK)
        mask = jnp.arange(end)[None, :] < t[:, None]
        z = jnp.einsum('bqhd,bkhd->bhqk', q[:, b * Q_BLOCK:end], k[:, :end],
                       preferred_element_type=jnp.float32)
        z = jnp.where(mask[None, None], z, -jnp.inf)
        sp = jnp.maximum(z, 0.0) + jnp.log1p(jnp.exp(-jnp.abs(z)))
        spb = sp.reshape(B, H, Q_BLOCK, nk, Q_BLOCK)
        within = jnp.einsum('bhqnj,js->bhqns', spb, later_in_block)
        tot = jnp.sum(spb, axis=-1)
        after = lax.cumsum(tot, axis=3, reverse=True) - tot
        later = (within + after[..., None]).reshape(B, H, Q_BLOCK, end)
        a = jnp.exp(z - (sp + later))
        o = jnp.einsum('bhqk,bkhd->bqhd', a.astype(v.dtype), v[:, :end],
                       preferred_element_type=jnp.float32)
        outs.append(o.astype(v.dtype))
    return jnp.concatenate(outs, axis=1)


def mla_mixer(h, w_in, q_norm, kv_norm, w_uq, w_ukv, q_gain, k_gain, w_o, cos, sin):
    B, S, _ = h.shape
    proj = h @ w_in
    c_q = rmsnorm(proj[..., :MLA_Q_LORA], q_norm)
    c_kv = rmsnorm(proj[..., MLA_Q_LORA:MLA_Q_LORA + MLA_KV_LORA], kv_norm)
    k_rope = proj[..., MLA_Q_LORA + MLA_KV_LORA:]
    q = (c_q @ w_uq).reshape(B, S, MLA_HEADS, MLA_QK_DIM)
    kv = (c_kv @ w_ukv).reshape(B, S, MLA_HEADS, MLA_NOPE + MLA_V)
    k_nope, v = kv[..., :MLA_NOPE], kv[..., MLA_NOPE:]
    k = jnp.concatenate(
        [k_nope, jnp.broadcast_to(k_rope[:, :, None, :], (B, S, MLA_HEADS, MLA_ROPE))], axis=-1)
    q = rope_tail(rmsnorm(q, q_gain), cos, sin) * (MLA_QK_DIM ** -0.5)
    k = rope_tail(rmsnorm(k, k_gain), cos, sin)
    o = chunk_causal_softmax_attention(q, k, v)
    return o.reshape(B, S, MLA_HEADS * MLA_V) @ w_o


def sb_mixer(h, w_qkv, w_o):
    B, S, _ = h.shape
    qkv = (h @ w_qkv).reshape(B, S, 3, SB_HEADS, SB_HEAD_DIM)
    q, k, v = qkv[:, :, 0] * (SB_HEAD_DIM ** -0.5), qkv[:, :, 1], qkv[:, :, 2]
    o = stick_breaking_attention(q, k, v)
    return o.reshape(B, S, SB_HEADS * SB_HEAD_DIM) @ w_o


def squared_relu_mlp(h, w_up, w_down):
    return jnp.square(jax.nn.relu(h @ w_up)) @ w_down


def setup_inputs(seed: int = 0) -> dict:
    key = jax.random.key(seed)
    ks = jax.random.split(key, 24)
    n_a = (DEPTH + 1) // 2
    n_b = DEPTH // 2

    def w(k, shape, fan_in):
        return jax.random.normal(k, shape, jnp.float32) * (fan_in ** -0.5)

    def gain(k, shape):
        return 1.0 + 0.02 * jax.random.normal(k, shape, jnp.float32)

    return {
        "x": jax.random.normal(ks[0], (BATCH, SEQ, D_MODEL), jnp.float32),
        "p": jax.random.normal(ks[1], (DEPTH, BATCH, SEQ, PLE_DIM), jnp.float32),
        "positions": jnp.broadcast_to(jnp.arange(SEQ, dtype=jnp.int32), (BATCH, SEQ)),
        "norm_mix": gain(ks[2], (DEPTH, D_MODEL)),
        "norm_mlp": gain(ks[3], (DEPTH, D_MODEL)),
        "norm_ple": gain(ks[4], (DEPTH, D_MODEL)),
        "mla_w_in": w(ks[5], (n_a, D_MODEL, MLA_Q_LORA + MLA_KV_LORA + MLA_ROPE), D_MODEL),
        "mla_q_norm": gain(ks[6], (n_a, MLA_Q_LORA)),
        "mla_kv_norm": gain(ks[7], (n_a, MLA_KV_LORA)),
        "mla_w_uq": w(ks[8], (n_a, MLA_Q_LORA, MLA_HEADS * MLA_QK_DIM), MLA_Q_LORA),
        "mla_w_ukv": w(ks[9], (n_a, MLA_KV_LORA, MLA_HEADS * (MLA_NOPE + MLA_V)), MLA_KV_LORA),
        "mla_q_gain": gain(ks[10], (n_a, MLA_QK_DIM)),
        "mla_k_gain": gain(ks[11], (n_a, MLA_QK_DIM)),
        "mla_w_o": w(ks[12], (n_a, MLA_HEADS * MLA_V, D_MODEL), MLA_HEADS * MLA_V),
        "sb_w_qkv": w(ks[13], (n_b, D_MODEL, 3 * SB_HEADS * SB_HEAD_DIM), D_MODEL),
        "sb_w_o": w(ks[14], (n_b, SB_HEADS * SB_HEAD_DIM, D_MODEL), SB_HEADS * SB_HEAD_DIM),
        "mlp_w_up": w(ks[15], (DEPTH, D_MODEL, D_FF), D_MODEL),
        "mlp_w_down": w(ks[16], (DEPTH, D_FF, D_MODEL), D_FF),
        "ple_w_proj": w(ks[17], (DEPTH, PLE_DIM, D_MODEL), PLE_DIM),
        "ple_w_gate": w(ks[18], (DEPTH, D_MODEL, D_MODEL), D_MODEL),
    }


def reference(x, p, positions, norm_mix, norm_mlp, norm_ple,
              mla_w_in, mla_q_norm, mla_kv_norm, mla_w_uq, mla_w_ukv,
              mla_q_gain, mla_k_gain, mla_w_o,
              sb_w_qkv, sb_w_o, mlp_w_up, mlp_w_down, ple_w_proj, ple_w_gate):
    cos, sin = rope_tables(positions, x.dtype)
    h = x
    for i in range(DEPTH):
        hn = rmsnorm(h, norm_mix[i])
        j = i // N_MIXERS
        if i % N_MIXERS == 0:
            mix = mla_mixer(hn, mla_w_in[j], mla_q_norm[j], mla_kv_norm[j], mla_w_uq[j],
                            mla_w_ukv[j], mla_q_gain[j], mla_k_gain[j], mla_w_o[j], cos, sin)
        else:
            mix = sb_mixer(hn, sb_w_qkv[j], sb_w_o[j])
        h = h + mix
        h = h + squared_relu_mlp(rmsnorm(h, norm_mlp[i]), mlp_w_up[i], mlp_w_down[i])
        gate = jax.nn.sigmoid(rmsnorm(h, norm_ple[i]) @ ple_w_gate[i])
        h = h + (p[i] @ ple_w_proj[i]) * gate
    return h
```

```python
import contextlib
import numpy as np
import ml_dtypes
import concourse.bass as bass
import concourse.mybir as mybir
from concourse.bass_utils import run_bass_kernel_spmd

F32 = mybir.dt.float32
BF16 = mybir.dt.bfloat16
I32 = mybir.dt.int32
AF = mybir.ActivationFunctionType
ALU = mybir.AluOpType

NCORES = 8
D = 4096
SEQ = 16384
T = SEQ // NCORES
NBLK = T // 128
DEPTH = 4
DFF = 4 * D
EPS = 1e-6
TT = 512


class Sem:
    def __init__(self, nc, name):
        self.h = nc.alloc_semaphore(name=name)
        self.n = 0
        self.base = 0

    def inc(self, ins, k=1):
        ins.then_inc(self.h, k)
        self.n += k
        return self.n

    def wait(self, eng, rel):
        eng.wait_ge(self.h, self.base + rel)

    def wait_all(self, eng):
        eng.wait_ge(self.h, self.n)


@contextlib.contextmanager
def stage_sems(nc, names):
    sems = [Sem(nc, uname(nm)) for nm in names]
    yield sems
    nc.clear_and_free_semaphores([sm.h for sm in sems])
    nc.all_engine_barrier()


_uid = [0]


def uname(p):
    _uid[0] += 1
    return f"{p}_{_uid[0]}"


def sb(es, nc, name, shape, dt):
    return es.enter_context(nc.sbuf_tensor(uname(name), list(shape), dt))


def ps(es, nc, name, shape, dt=F32):
    return es.enter_context(nc.psum_tensor(uname(name), list(shape), dt))


def gemm_stage(nc, xT, w, out, K, N, Tn, mode="FM", epi="copy", aux=(), kcs=None):
    KC = K // 128
    assert K % 128 == 0
    if kcs is None:
        kcs = 32 if KC <= 32 else 16
    kcs = min(kcs, KC)
    KG = KC // kcs
    assert KC % kcs == 0
    NB = (N + 511) // 512
    NTT = Tn // TT
    assert Tn % TT == 0
    xdouble = KC <= 32
    out_dt = out.dtype
    n_aux = len(aux)
    xv = xT.rearrange("(kc p) t -> p kc t", p=128)
    wv = w.rearrange("(kc p) n -> p kc n", p=128)

    with contextlib.ExitStack() as es:
        xbufs = [sb(es, nc, "gx", [128, KC, TT], BF16) for _ in range(2 if xdouble else 1)]
        wbufs = [sb(es, nc, "gw", [128, kcs, 512], BF16) for _ in range(2)]
        NOB = 4
        obufs = [sb(es, nc, "go", [128, 512], out_dt) for _ in range(NOB)]
        tbufs = [sb(es, nc, "gt", [128, 512], F32) for _ in range(NOB)] if epi in ("relu2", "ple") else None
        abufs = [[sb(es, nc, "ga", [128, 512], F32) for _ in range(NOB)] for _ in range(n_aux)]
        pst = ps(es, nc, "gps", [128, 8, 512])
        NX = 2 if xdouble else 1
        allsems = es.enter_context(stage_sems(
            nc, ["pe", "act", "dve"] + [f"x{i}" for i in range(NX)] + ["w0", "w1"] +
            [f"a{i}" for i in range(4)] + [f"s{i}" for i in range(4)]))
        pe_d, act_d, dve_d = allsems[0:3]
        x_ld = allsems[3:3 + NX]
        w_ld = allsems[3 + NX:5 + NX]
        aux_ld = allsems[5 + NX:9 + NX]
        st_d = allsems[9 + NX:13 + NX]
        block = es.enter_context(nc.Block())

        subs = []
        blocks = []
        for tt in range(NTT):
            for nb in range(NB):
                nw = min(512, N - nb * 512)
                if mode == "FM":
                    nsub = (nw + 127) // 128
                    widths = [min(128, nw - i * 128) for i in range(nsub)]
                else:
                    nsub = 4
                    widths = [128] * 4
                first = len(subs)
                for i in range(nsub):
                    subs.append(dict(tt=tt, nb=nb, i=i, pw=widths[i], nw=nw))
                for kg in range(KG):
                    blocks.append(dict(tt=tt, nb=nb, kg=kg, nw=nw, first=first, nsub=nsub))
        cnt = 0
        for b in blocks:
            b["pe_end"] = []
            for i in range(b["nsub"]):
                cnt += 1
                b["pe_end"].append(cnt)
                if b["kg"] == KG - 1:
                    subs[b["first"] + i]["pe_ready"] = cnt
        n_wdma = (kcs + 7) // 8
        n_xdma = (KC + 7) // 8

        def out_ap(sd):
            tt, nb, i, pw, nw = sd["tt"], sd["nb"], sd["i"], sd["pw"], sd["nw"]
            if mode == "FM":
                r0 = nb * 512 + i * 128
                return (lambda t: t[r0:r0 + pw, tt * TT:(tt + 1) * TT]), pw, TT
            else:
                r0 = tt * TT + i * 128
                return (lambda t: t[r0:r0 + 128, nb * 512:nb * 512 + nw]), 128, nw

        @block.gpsimd
        def _(g):
            def load_x(tt):
                xb = xbufs[tt % len(xbufs)]
                prev = tt - len(xbufs)
                if prev >= 0:
                    last = [b for b in blocks if b["tt"] == prev][-1]
                    pe_d.wait(g, last["pe_end"][-1])
                for q in range(n_xdma):
                    x_ld[tt % NX].inc(g.dma_start(out=xb[:, q * 8:min(KC, (q + 1) * 8), :],
                                         in_=xv[:, q * 8:min(KC, (q + 1) * 8), tt * TT:(tt + 1) * TT]), 16)

            load_x(0)
            for bi, b in enumerate(blocks):
                wb = wbufs[bi % 2]
                if bi >= 2:
                    pe_d.wait(g, blocks[bi - 2]["pe_end"][-1])
                nw = b["nw"]
                for q in range(n_wdma):
                    k0 = b["kg"] * kcs + q * 8
                    k1 = b["kg"] * kcs + min(kcs, (q + 1) * 8)
                    w_ld[bi % 2].inc(g.dma_start(out=wb[:, q * 8:q * 8 + (k1 - k0), 0:nw],
                                         in_=wv[:, k0:k1, b["nb"] * 512:b["nb"] * 512 + nw]), 16)
                if xdouble:
                    if b["nb"] == 0 and b["kg"] == 0 and b["tt"] + 1 < NTT:
                        load_x(b["tt"] + 1)
                else:
                    if b["nb"] == NB - 1 and b["kg"] == KG - 1 and b["tt"] + 1 < NTT:
                        load_x(b["tt"] + 1)

        @block.tensor
        def _(pe):
            for bi, b in enumerate(blocks):
                tt = b["tt"]
                xb = xbufs[tt % len(xbufs)]
                wb = wbufs[bi % 2]
                if b["nb"] == 0 and b["kg"] == 0:
                    x_ld[tt % NX].wait(pe, 16 * n_xdma * (tt // NX + 1))
                w_ld[bi % 2].wait(pe, 16 * n_wdma * (bi // 2 + 1))
                for i in range(b["nsub"]):
                    s = b["first"] + i
                    bank = s % 8
                    if b["kg"] == 0 and s >= 8:
                        dve_d.wait(pe, s - 7)
                    pw = subs[s]["pw"]
                    for kc in range(kcs):
                        kk = b["kg"] * kcs + kc
                        if mode == "FM":
                            mm = pe.matmul(pst[0:pw, bank, :], lhsT=wb[:, kc, i * 128:i * 128 + pw],
                                           rhs=xb[:, kk, :], start=(kk == 0), stop=(kk == KC - 1))
                        else:
                            mm = pe.matmul(pst[:, bank, 0:b["nw"]], lhsT=xb[:, kk, i * 128:(i + 1) * 128],
                                           rhs=wb[:, kc, 0:b["nw"]], start=(kk == 0), stop=(kk == KC - 1))
                    pe_d.inc(mm, 1)

        if epi in ("relu2", "ple"):
            @block.scalar
            def _(act):
                for s, sd in enumerate(subs):
                    _, pp, ff = out_ap(sd)
                    pe_d.wait(act, sd["pe_ready"])
                    if s >= NOB:
                        dve_d.wait(act, s - NOB + 1)
                    func = AF.Square if epi == "relu2" else AF.Sigmoid
                    act_d.inc(act.activation(out=tbufs[s % NOB][0:pp, 0:ff], in_=pst[0:pp, s % 8, 0:ff], func=func), 1)

        @block.vector
        def _(v):
            for s, sd in enumerate(subs):
                _, pp, ff = out_ap(sd)
                o = obufs[s % NOB][0:pp, 0:ff]
                pv = pst[0:pp, s % 8, 0:ff]
                pe_d.wait(v, sd["pe_ready"])
                if s >= NOB:
                    st_d[s % NOB].wait(v, 16 * (s // NOB))
                if n_aux:
                    aux_ld[s % NOB].wait(v, 16 * n_aux * (s // NOB + 1))
                if epi == "copy":
                    ins = v.tensor_copy(out=o, in_=pv)
                elif epi == "relu2":
                    act_d.wait(v, s + 1)
                    ins = v.scalar_tensor_tensor(out=o, in0=pv, scalar=0.0, in1=tbufs[s % NOB][0:pp, 0:ff],
                                                 op0=ALU.is_gt, op1=ALU.mult)
                elif epi == "resid":
                    ins = v.tensor_tensor(out=o, in0=pv, in1=abufs[0][s % NOB][0:pp, 0:ff], op=ALU.add)
                elif epi == "ple":
                    act_d.wait(v, s + 1)
                    tb = tbufs[s % NOB][0:pp, 0:ff]
                    v.tensor_tensor(out=tb, in0=tb, in1=abufs[1][s % NOB][0:pp, 0:ff], op=ALU.mult)
                    ins = v.tensor_tensor(out=o, in0=tb, in1=abufs[0][s % NOB][0:pp, 0:ff], op=ALU.add)
                else:
                    raise ValueError(epi)
                dve_d.inc(ins, 1)

        @block.sync
        def _(sy):
            LOOK = 2

            def aux_load(s):
                if s >= len(subs) or not n_aux:
                    return
                sel, pp, ff = out_ap(subs[s])
                if s >= NOB:
                    dve_d.wait(sy, s - NOB + 1)
                for a in range(n_aux):
                    aux_ld[s % NOB].inc(sy.dma_start(out=abufs[a][s % NOB][0:pp, 0:ff], in_=sel(aux[a])), 16)

            for s in range(min(LOOK, len(subs))):
                aux_load(s)
            for s, sd in enumerate(subs):
                aux_load(s + LOOK)
                sel, pp, ff = out_ap(sd)
                dve_d.wait(sy, s + 1)
                st_d[s % NOB].inc(sy.dma_start(out=sel(out), in_=obufs[s % NOB][0:pp, 0:ff]), 16)
            for sm in st_d:
                sm.wait_all(sy)


class Sched:
    def __init__(self, nc):
        self.nc = nc
        self.items = []
        self.tots = []

    def add(self, eng, fn, waits, inc, tot=0):
        self.items.append((eng, waits, fn, inc))
        self.tots.append(tot)

    def check(self):
        q = {}
        for idx, it in enumerate(self.items):
            q.setdefault(it[0], []).append(idx)
        ptr = {e: 0 for e in q}
        cnt = {}
        prog = True
        while prog:
            prog = False
            for e, lst in q.items():
                while ptr[e] < len(lst):
                    it = self.items[lst[ptr[e]]]
                    if all(cnt.get(id(sm), 0) >= val for sm, val in it[1]):
                        if it[3] is not None:
                            cnt[id(it[3][0])] = cnt.get(id(it[3][0]), 0) + self.tots[lst[ptr[e]]]
                        ptr[e] += 1
                        prog = True
                    else:
                        break
        for e, lst in q.items():
            if ptr[e] < len(lst):
                it = self.items[lst[ptr[e]]]
                raise RuntimeError(f"schedule deadlock: engine {e} stuck at item {ptr[e]}/{len(lst)} waits="
                                   f"{[(sm.h, val, cnt.get(id(sm), 0)) for sm, val in it[1]]}")

    def emit(self, block, lanes):
        self.check()
        for ename in ("scalar", "gpsimd", "vector", "tensor", "sync"):
            its = [it for it in self.items if it[0] == ename]
            if not its and ename != "sync":
                continue

            def body(eng, its=its, ename=ename):
                for (_, waits, fn, inc) in its:
                    for (sm, val) in waits:
                        if val > 0:
                            eng.wait_ge(sm.h, val)
                    r = fn(eng)
                    if inc is not None:
                        sm, k = inc
                        if isinstance(r, (list, tuple)):
                            for ins in r:
                                sm.inc(ins, k)
                        else:
                            sm.inc(r, k)
                if ename == "sync":
                    for ln in lanes:
                        assert ln.sem.n == ln.plan, (ln.sem.n, ln.plan)
                        eng.wait_ge(ln.sem.h, ln.plan)

            getattr(block, ename)(body)


class Lane:
    def __init__(self, sched, sem):
        self.sched = sched
        self.sem = sem
        self.plan = 0

    def step(self, eng, fn, n=1, extra=()):
        k = 16 if eng in ("sync", "dma_act", "dma_pool") else 1
        real = {"dma_act": "scalar", "dma_pool": "gpsimd"}.get(eng, eng)
        waits = [(self.sem, self.plan)] + [w for w in extra if w is not None]
        self.sched.add(real, fn, waits, (self.sem, k), tot=k * n)
        self.plan += k * n
        assert self.plan < 32000, "semaphore count limit"

    def mark(self):
        return (self.sem, self.plan)


def _chunks(n, c):
    return [(i, min(n, i + c)) for i in range(0, n, c)]


def norm_stage(nc, inT, g2d, outT, R, Tn, TW=256):
    RC = R // 128
    NT_ = Tn // TW
    NL = 2
    inv = inT.rearrange("(rc p) t -> p rc t", p=128)
    outv = outT.rearrange("(rc p) t -> p rc t", p=128)
    with contextlib.ExitStack() as es:
        hb = [sb(es, nc, "nh", [128, RC, TW], F32) for _ in range(NL)]
        sq = [sb(es, nc, "nsq", [128, RC, TW], BF16) for _ in range(NL)]
        ob = [sb(es, nc, "no", [128, RC, TW], BF16) for _ in range(NL)]
        sd = [sb(es, nc, "nsd", [128, TW], F32) for _ in range(NL)]
        rs = [sb(es, nc, "nrs", [128, TW], F32) for _ in range(NL)]
        gt = sb(es, nc, "ng", [128, RC], F32)
        ones = sb(es, nc, "nones", [128, 128], BF16)
        epst = sb(es, nc, "neps", [128, 1], F32)
        pst = ps(es, nc, "nps", [128, NL, 512])
        sems = es.enter_context(stage_sems(nc, ["c0", "l0", "l1"]))
        block = es.enter_context(nc.Block())
        sch = Sched(nc)
        c0 = Lane(sch, sems[0])
        lanes = [Lane(sch, sems[1]), Lane(sch, sems[2])]
        c0.step("sync", lambda e: e.dma_start(out=gt[:, :], in_=g2d))
        c0.step("vector", lambda e: e.memset(ones[:, :], 1.0))
        c0.step("vector", lambda e: e.memset(epst[:, :], EPS))
        ready = c0.mark()
        for t in range(NT_):
            L = lanes[t % NL]
            b = t % NL
            ts = slice(t * TW, (t + 1) * TW)
            ch = _chunks(RC, 8)
            L.step("sync", lambda e, b=b, ts=ts, ch=ch: [e.dma_start(out=hb[b][:, a:z, :], in_=inv[:, a:z, ts]) for a, z in ch], n=len(ch))
            L.step("scalar", lambda e, b=b: e.activation(out=sq[b][:, :, :], in_=hb[b][:, :, :], func=AF.Square))

            def mm(e, b=b):
                for rc in range(RC):
                    ins = e.matmul(pst[:, b, 0:TW], lhsT=ones[:, :], rhs=sq[b][:, rc, :], start=(rc == 0), stop=(rc == RC - 1))
                return ins
            L.step("tensor", mm, extra=[ready])
            L.step("scalar", lambda e, b=b: e.activation(out=sd[b][:, :], in_=pst[:, b, 0:TW], func=AF.Sqrt, bias=epst[:, 0:1], scale=1.0 / R), extra=[ready])
            L.step("vector", lambda e, b=b: e.reciprocal(out=rs[b][:, :], in_=sd[b][:, :]))

            def sc(e, b=b):
                for rc in range(RC):
                    ins = e.scalar_tensor_tensor(out=ob[b][:, rc, :], in0=hb[b][:, rc, :], scalar=gt[:, rc:rc + 1],
                                                 in1=rs[b][:, :], op0=ALU.mult, op1=ALU.mult)
                return ins
            L.step("vector", sc)
            L.step("sync", lambda e, b=b, ts=ts, ch=ch: [e.dma_start(out=outv[:, a:z, ts], in_=ob[b][:, a:z, :]) for a, z in ch], n=len(ch))
        sch.emit(block, [c0] + lanes)


MAGIC = 12582912.0
TWO_PI = 6.283185307179586
CW1 = 6.28125
CW2 = float(np.float32(TWO_PI - CW1))
CW3 = float(TWO_PI - CW1 - float(np.float32(TWO_PI - CW1)))
PI_LIM = 3.1415925


def rope_stage(nc, posrep, consts, cos2T, sin2sT, Tn):
    with contextlib.ExitStack() as es:
        pi_ = sb(es, nc, "rpi", [64, Tn], I32)
        ang = sb(es, nc, "rang", [64, Tn], F32)
        kk = sb(es, nc, "rk", [64, Tn], F32)
        rr = sb(es, nc, "rr", [64, Tn], F32)
        oo = sb(es, nc, "ro", [64, Tn], F32)
        cst = sb(es, nc, "rc", [64, 2], F32)
        sems = es.enter_context(stage_sems(nc, ["c0"]))
        block = es.enter_context(nc.Block())
        sch = Sched(nc)
        L = Lane(sch, sems[0])
        L.step("sync", lambda e: e.dma_start(out=pi_[:, :], in_=posrep))
        L.step("sync", lambda e: e.dma_start(out=cst[:, :], in_=consts))
        L.step("vector", lambda e: e.tensor_copy(out=ang[:, :], in_=pi_[:, :]))
        L.step("vector", lambda e: e.tensor_scalar(out=ang[:, :], in0=ang[:, :], scalar1=cst[:, 0:1], scalar2=None, op0=ALU.mult))
        for which in ("sin", "cos"):
            shift = 0.0 if which == "sin" else 0.25
            L.step("vector", lambda e, shift=shift: e.tensor_scalar(out=kk[:, :], in0=ang[:, :], scalar1=1.0 / TWO_PI, scalar2=shift,
                                                                 op0=ALU.mult, op1=ALU.add))
            L.step("vector", lambda e: e.tensor_scalar(out=kk[:, :], in0=kk[:, :], scalar1=MAGIC, scalar2=None, op0=ALU.add))
            L.step("vector", lambda e: e.tensor_scalar(out=kk[:, :], in0=kk[:, :], scalar1=-MAGIC, scalar2=None, op0=ALU.add))
            L.step("vector", lambda e: e.scalar_tensor_tensor(out=rr[:, :], in0=kk[:, :], scalar=-CW1, in1=ang[:, :], op0=ALU.mult, op1=ALU.add))
            L.step("vector", lambda e: e.scalar_tensor_tensor(out=rr[:, :], in0=kk[:, :], scalar=-CW2, in1=rr[:, :], op0=ALU.mult, op1=ALU.add))
            L.step("vector", lambda e: e.scalar_tensor_tensor(out=rr[:, :], in0=kk[:, :], scalar=-CW3, in1=rr[:, :], op0=ALU.mult, op1=ALU.add))
            if which == "cos":
                L.step("vector", lambda e: e.tensor_scalar(out=rr[:, :], in0=rr[:, :], scalar1=TWO_PI / 4, scalar2=None, op0=ALU.add))
            L.step("vector", lambda e: e.tensor_scalar(out=rr[:, :], in0=rr[:, :], scalar1=PI_LIM, scalar2=-PI_LIM, op0=ALU.min, op1=ALU.max))
            L.step("scalar", lambda e: e.activation(out=oo[:, :], in_=rr[:, :], func=AF.Sin))
            if which == "sin":
                L.step("vector", lambda e: e.tensor_scalar(out=oo[:, :], in0=oo[:, :], scalar1=cst[:, 1:2], scalar2=None, op0=ALU.mult))
                L.step("sync", lambda e: e.dma_start(out=sin2sT, in_=oo[:, :]))
            else:
                L.step("sync", lambda e: e.dma_start(out=cos2T, in_=oo[:, :]))
        sch.emit(block, [L])


def qkprep_stage(nc, qT, knT, projT, gains, cos2T, sin2sT, QTn, QTr, KTn, KTr, Tn, NH=8):
    NL = 2
    with contextlib.ExitStack() as es:
        nb = [sb(es, nc, "qn", [128, TT], F32) for _ in range(NL)]
        rb = [sb(es, nc, "qr", [64, TT], F32) for _ in range(NL)]
        wb = [sb(es, nc, "qw", [64, TT], F32) for _ in range(NL)]
        sqn = [sb(es, nc, "qsn", [128, TT], BF16) for _ in range(NL)]
        sqr = [sb(es, nc, "qsr", [64, TT], BF16) for _ in range(NL)]
        sd = [sb(es, nc, "qsd", [128, TT], F32) for _ in range(NL)]
        rs = [sb(es, nc, "qrs", [128, TT], F32) for _ in range(NL)]
        on = [sb(es, nc, "qon", [128, TT], BF16) for _ in range(NL)]
        orr = [sb(es, nc, "qor", [64, TT], BF16) for _ in range(NL)]
        kr = sb(es, nc, "qkr", [64, TT], F32)
        kw = sb(es, nc, "qkw", [64, TT], F32)
        cs = sb(es, nc, "qcs", [64, TT], F32)
        sn = sb(es, nc, "qsn2", [64, TT], F32)
        gt = sb(es, nc, "qg", [128, 6], F32)
        ones = sb(es, nc, "qones", [128, 128], BF16)
        bq = sb(es, nc, "qbq", [128, 1], F32)
        bk = sb(es, nc, "qbk", [128, 1], F32)
        pst = ps(es, nc, "qps", [128, NL, 512])
        sems = es.enter_context(stage_sems(nc, ["c0", "l0", "l1"]))
        block = es.enter_context(nc.Block())
        sch = Sched(nc)
        c0 = Lane(sch, sems[0])
        lanes = [Lane(sch, sems[1]), Lane(sch, sems[2])]
        c0.step("sync", lambda e: e.dma_start(out=gt[:, :], in_=gains))
        c0.step("vector", lambda e: e.memset(ones[:, :], 1.0))
        c0.step("vector", lambda e: e.memset(bq[:, :], 192.0 * EPS))
        c0.step("vector", lambda e: e.memset(bk[:, :], EPS))
        u = 0
        for tt in range(Tn // TT):
            ts = slice(tt * TT, (tt + 1) * TT)
            c0.step("sync", lambda e, ts=ts: [e.dma_start(out=kr[:, :], in_=projT[1536:1600, ts]),
                                               e.dma_start(out=kw[:, :], in_=projT[1600:1664, ts]),
                                               e.dma_start(out=cs[:, :], in_=cos2T[:, ts]),
                                               e.dma_start(out=sn[:, :], in_=sin2sT[:, ts])], n=4,
                    extra=[lanes[0].mark(), lanes[1].mark()])
            ready = c0.mark()
            for h in range(NH):
                for which in ("q", "k"):
                    L = lanes[u % NL]
                    b = u % NL
                    u += 1
                    isq = which == "q"
                    gc = 0 if isq else 3
                    if isq:
                        L.step("sync", lambda e, b=b, h=h, ts=ts: [
                            e.dma_start(out=nb[b][:, :], in_=qT[h * 128:(h + 1) * 128, ts]),
                            e.dma_start(out=rb[b][:, :], in_=qT[1024 + h * 64:1024 + (h + 1) * 64, ts]),
                            e.dma_start(out=wb[b][:, :], in_=qT[1536 + h * 64:1536 + (h + 1) * 64, ts])], n=3)
                        rsrc, wsrc = rb[b], wb[b]
                    else:
                        L.step("sync", lambda e, b=b, h=h, ts=ts: e.dma_start(out=nb[b][:, :], in_=knT[h * 128:(h + 1) * 128, ts]))
                        rsrc, wsrc = kr, kw
                    L.step("scalar", lambda e, b=b, rsrc=rsrc: [e.activation(out=sqn[b][:, :], in_=nb[b][:, :], func=AF.Square),
                                                                e.activation(out=sqr[b][:, :], in_=rsrc[:, :], func=AF.Square)], n=2,
                           extra=[ready])

                    def mm(e, b=b):
                        e.matmul(pst[:, b, :], lhsT=ones[:, :], rhs=sqn[b][:, :], start=True, stop=False)
                        return e.matmul(pst[:, b, :], lhsT=ones[0:64, :], rhs=sqr[b][:, :], start=False, stop=True)
                    L.step("tensor", mm)
                    if isq:
                        L.step("scalar", lambda e, b=b: e.activation(out=sd[b][:, :], in_=pst[:, b, :], func=AF.Sqrt, bias=bq[:, 0:1], scale=1.0))
                    else:
                        L.step("scalar", lambda e, b=b: e.activation(out=sd[b][:, :], in_=pst[:, b, :], func=AF.Sqrt, bias=bk[:, 0:1], scale=1.0 / 192.0))
                    L.step("vector", lambda e, b=b: e.reciprocal(out=rs[b][:, :], in_=sd[b][:, :]))
                    L.step("vector", lambda e, b=b, gc=gc: e.scalar_tensor_tensor(out=on[b][:, :], in0=nb[b][:, :], scalar=gt[:, gc:gc + 1],
                                                                                 in1=rs[b][:, :], op0=ALU.mult, op1=ALU.mult))
                    L.step("vector", lambda e, b=b, gc=gc, rsrc=rsrc, wsrc=wsrc: [
                        e.scalar_tensor_tensor(out=rb[b][:, :], in0=rsrc[:, :], scalar=gt[0:64, gc + 1:gc + 2], in1=rs[b][0:64, :], op0=ALU.mult, op1=ALU.mult),
                        e.scalar_tensor_tensor(out=wb[b][:, :], in0=wsrc[:, :], scalar=gt[0:64, gc + 2:gc + 3], in1=rs[b][0:64, :], op0=ALU.mult, op1=ALU.mult)], n=2)
                    L.step("vector", lambda e, b=b: [e.tensor_tensor(out=rb[b][:, :], in0=rb[b][:, :], in1=cs[:, :], op=ALU.mult),
                                                     e.tensor_tensor(out=wb[b][:, :], in0=wb[b][:, :], in1=sn[:, :], op=ALU.mult)], n=2)
                    L.step("vector", lambda e, b=b: e.tensor_tensor(out=orr[b][:, :], in0=rb[b][:, :], in1=wb[b][:, :], op=ALU.add))
                    dn, dr = (QTn, QTr) if isq else (KTn, KTr)
                    L.step("sync", lambda e, b=b, h=h, ts=ts, dn=dn, dr=dr: [
                        e.dma_start(out=dn[h * 128:(h + 1) * 128, ts], in_=on[b][:, :]),
                        e.dma_start(out=dr[h * 64:(h + 1) * 64, ts], in_=orr[b][:, :])], n=2)
        sch.emit(block, [c0] + lanes)


def mla_attn_head(nc, h, QTn, QTr, KTn_g, KTr_g, V_g, mask, oT, Tn):
    NBL = Tn // 128
    NG = Tn // 512
    NL = 4
    D_ = NL - 1
    knv = KTn_g.rearrange("(r x) t -> x r t", r=NCORES)
    krv = KTr_g.rearrange("(r x) t -> x r t", r=NCORES)
    vv = V_g.rearrange("(b s) n -> s b n", s=128)
    with contextlib.ExitStack() as es:
        kn = sb(es, nc, "akn", [128, NCORES, Tn], BF16)
        kr = sb(es, nc, "akr", [64, NCORES, Tn], BF16)
        v = sb(es, nc, "av", [128, NCORES * NBL, 128], BF16)
        qn = sb(es, nc, "aqn", [128, Tn], BF16)
        qr = sb(es, nc, "aqr", [64, Tn], BF16)
        mk = sb(es, nc, "amk", [128, 32, 512], BF16)
        ones = sb(es, nc, "aones", [128, 128], BF16)
        P = [sb(es, nc, "aP", [128, 512], BF16) for _ in range(NL)]
        rden = sb(es, nc, "ard", [128, 512], F32)
        ob = sb(es, nc, "aob", [128, 512], BF16)
        pst = ps(es, nc, "aps", [128, 8, 512])
        sems = es.enter_context(stage_sems(nc, ["ld", "fin"] + [f"l{i}" for i in range(NL)]))
        block = es.enter_context(nc.Block())
        sch = Sched(nc)
        ld = Lane(sch, sems[0])
        fin = Lane(sch, sems[1])
        lanes = [Lane(sch, sems[2 + i]) for i in range(NL)]
        ld.step("sync", lambda e: [e.dma_start(out=kn[:, r, :], in_=knv[h * 128:(h + 1) * 128, r, :]) for r in range(NCORES)], n=NCORES)
        ld.step("dma_act", lambda e: [e.dma_start(out=kr[:, r, :], in_=krv[h * 64:(h + 1) * 64, r, :]) for r in range(NCORES)], n=NCORES)
        ld.step("sync", lambda e: [e.dma_start(out=v[:, r * NBL:(r + 1) * NBL, :], in_=vv[:, r * NBL:(r + 1) * NBL, h * 128:(h + 1) * 128])
                                   for r in range(NCORES)], n=NCORES)
        ld.step("dma_act", lambda e: [e.dma_start(out=qn[:, :], in_=QTn[h * 128:(h + 1) * 128, :]),
                                      e.dma_start(out=qr[:, :], in_=QTr[h * 64:(h + 1) * 64, :]),
                                      e.dma_start(out=mk[:, :, :], in_=mask)], n=3)
        ld.step("vector", lambda e: e.memset(ones[:, :], 1.0))
        ready = ld.mark()

        kbs = []
        for g in range(NG):
            lst = [(ip, r) for ip in range(4 * g + 4) for r in range(NCORES)]
            for j, (ip, r) in enumerate(lst):
                kbs.append(dict(g=g, ip=ip, r=r, first=(j == 0), last=(j == len(lst) - 1),
                                m=((ip - 4 * g) * NCORES + r) if ip >= 4 * g else None))
        fin_marks = {}

        def add_front(i):
            kb = kbs[i]
            L = lanes[i % NL]
            l = i % NL
            g, ip, r = kb["g"], kb["ip"], kb["r"]
            qs = slice(g * 512, (g + 1) * 512)
            ks = slice(ip * 128, (ip + 1) * 128)

            def smm(e):
                e.matmul(pst[:, l, :], lhsT=kn[:, r, ks], rhs=qn[:, qs], start=True, stop=False)
                return e.matmul(pst[:, l, :], lhsT=kr[:, r, ks], rhs=qr[:, qs], start=False, stop=True)
            L.step("tensor", smm, extra=[ready])
            L.step("scalar", lambda e: e.activation(out=P[l][:, :], in_=pst[:, l, :], func=AF.Exp))
            if kb["m"] is not None:
                L.step("vector", lambda e: e.tensor_tensor(out=P[l][:, :], in0=P[l][:, :], in1=mk[:, kb["m"], :], op=ALU.mult))

        def add_back(i):
            kb = kbs[i]
            L = lanes[i % NL]
            l = i % NL
            g, ip, r = kb["g"], kb["ip"], kb["r"]
            par = g % 2

            def pv(e):
                e.matmul(pst[:, 4 + par, :], lhsT=v[:, r * NBL + ip, :], rhs=P[l][:, :], start=kb["first"], stop=kb["last"])
                return e.matmul(pst[:, 6 + par, :], lhsT=ones[:, :], rhs=P[l][:, :], start=kb["first"], stop=kb["last"])
            extra = []
            if kb["first"] and g >= 2:
                extra.append(fin_marks[g - 2])
            L.step("tensor", pv, extra=extra)
            if kb["last"]:
                fin.step("vector", lambda e: e.reciprocal(out=rden[:, :], in_=pst[:, 6 + par, :]), extra=[L.mark()])
                fin.step("vector", lambda e: e.tensor_tensor(out=ob[:, :], in0=pst[:, 4 + par, :], in1=rden[:, :], op=ALU.mult))
                fin_marks[g] = fin.mark()
                fin.step("sync", lambda e: e.dma_start(out=oT[h * 128:(h + 1) * 128, g * 512:(g + 1) * 512], in_=ob[:, :]))

        for i in range(len(kbs)):
            add_front(i)
            if i >= D_:
                add_back(i - D_)
        for i in range(max(0, len(kbs) - D_), len(kbs)):
            add_back(i)
        sch.emit(block, [ld, fin] + lanes)


def sb_attn_head(nc, h, QT, KT_g, V_g, mask, lincl, oT, Tn):
    NBL = Tn // 128
    NG = Tn // 512
    NL = 2
    ktv = KT_g.rearrange("(r x) t -> x r t", r=NCORES)
    vv = V_g.rearrange("(b s) n -> s b n", s=128)
    with contextlib.ExitStack() as es:
        kt = sb(es, nc, "skt", [128, 2, NCORES, Tn], BF16)
        v = sb(es, nc, "sv", [128, NCORES * NBL, 256], BF16)
        q = sb(es, nc, "sq", [128, 2, Tn], BF16)
        mk = sb(es, nc, "smk", [128, 32, 512], BF16)
        li = sb(es, nc, "sli", [128, 128], BF16)
        ones = sb(es, nc, "sones", [128, 128], BF16)
        one1 = sb(es, nc, "sone1", [128, 1], F32)
        e_ = [sb(es, nc, "se", [128, 512], F32) for _ in range(NL)]
        sp = [sb(es, nc, "ssp", [128, 512], BF16) for _ in range(NL)]
        ew = [sb(es, nc, "sew", [128, 512], F32) for _ in range(NL)]
        P = [sb(es, nc, "sP", [128, 512], BF16) for _ in range(NL)]
        Sf = sb(es, nc, "sSf", [128, 512], F32)
        Sb = [sb(es, nc, "sSb", [128, 512], BF16) for _ in range(2)]
        ob = [sb(es, nc, "sob", [128, 512], BF16) for _ in range(2)]
        pst = ps(es, nc, "sps", [128, 8, 512])
        sems = es.enter_context(stage_sems(nc, ["ld", "fin", "sl", "l0", "l1"]))
        block = es.enter_context(nc.Block())
        sch = Sched(nc)
        ld = Lane(sch, sems[0])
        fin = Lane(sch, sems[1])
        SL = Lane(sch, sems[2])
        lanes = [Lane(sch, sems[3]), Lane(sch, sems[4])]
        for dh in range(2):
            ld.step("sync" if dh == 0 else "dma_act", lambda e, dh=dh: [
                e.dma_start(out=kt[:, dh, r, :], in_=ktv[h * 256 + dh * 128:h * 256 + (dh + 1) * 128, r, :]) for r in range(NCORES)], n=NCORES)
        ld.step("sync", lambda e: [e.dma_start(out=v[:, r * NBL:(r + 1) * NBL, :], in_=vv[:, r * NBL:(r + 1) * NBL, h * 256:(h + 1) * 256])
                                   for r in range(NCORES)], n=NCORES)
        ld.step("dma_act", lambda e: [e.dma_start(out=q[:, 0, :], in_=QT[h * 256:h * 256 + 128, :]),
                                      e.dma_start(out=q[:, 1, :], in_=QT[h * 256 + 128:h * 256 + 256, :]),
                                      e.dma_start(out=mk[:, :, :], in_=mask),
                                      e.dma_start(out=li[:, :], in_=lincl)], n=4)
        ld.step("vector", lambda e: [e.memset(ones[:, :], 1.0), e.memset(one1[:, :], 1.0)], n=2)
        ready = ld.mark()

        kbs = []
        for g in range(NG):
            lst = [(ip, r) for ip in range(4 * g + 3, -1, -1) for r in range(NCORES - 1, -1, -1)]
            for j, (ip, r) in enumerate(lst):
                kbs.append(dict(g=g, ip=ip, r=r, j=j, first=(j == 0), last=(j == len(lst) - 1),
                                m=((ip - 4 * g) * NCORES + r) if ip >= 4 * g else None))
        fin_marks = {}
        wmark = {}
        smark = {}

        def add_A(i):
            kb = kbs[i]
            L = lanes[i % NL]
            l = i % NL
            g, ip, r = kb["g"], kb["ip"], kb["r"]
            qs = slice(g * 512, (g + 1) * 512)
            ks = slice(ip * 128, (ip + 1) * 128)

            def zmm(e):
                e.matmul(pst[:, l, :], lhsT=kt[:, 0, r, ks], rhs=q[:, 0, qs], start=True, stop=False)
                return e.matmul(pst[:, l, :], lhsT=kt[:, 1, r, ks], rhs=q[:, 1, qs], start=False, stop=True)
            L.step("tensor", zmm, extra=[ready])
            L.step("scalar", lambda e: e.activation(out=e_[l][:, :], in_=pst[:, l, :], func=AF.Exp, scale=1.0 / 16.0))
            L.step("scalar", lambda e: e.activation(out=sp[l][:, :], in_=e_[l][:, :], func=AF.Ln, bias=one1[:, 0:1], scale=1.0),
                   extra=[smark.get(i - NL)])
            if kb["m"] is not None:
                L.step("vector", lambda e: e.tensor_tensor(out=sp[l][:, :], in0=sp[l][:, :], in1=mk[:, kb["m"], :], op=ALU.mult))

        def add_S(i):
            kb = kbs[i]
            l = i % NL
            j = kb["j"]
            if kb["first"]:
                ex = [wmark[i - 1]] if i > 0 else []
                SL.step("gpsimd", lambda e: [e.memset(Sf[:, :], 0.0), e.memset(Sb[0][:, :], 0.0), e.memset(Sb[1][:, :], 0.0)], n=3, extra=ex)
            SL.step("gpsimd", lambda e: e.tensor_tensor(out=Sf[:, :], in0=Sf[:, :], in1=sp[l][:, :], op=ALU.add), extra=[lanes[l].mark()])
            ex = [wmark[i - 1]] if (i > 0 and not kb["first"]) else []
            SL.step("gpsimd", lambda e: e.tensor_copy(out=Sb[(j + 1) % 2][:, :], in_=Sf[:, :]), extra=ex)
            smark[i] = SL.mark()

        def add_B(i):
            kb = kbs[i]
            L = lanes[i % NL]
            l = i % NL
            g, ip, r, j = kb["g"], kb["ip"], kb["r"], kb["j"]
            par = g % 2

            def wmm(e):
                ins = e.matmul(pst[:, 2 + l, :], lhsT=li[:, :], rhs=sp[l][:, :], start=True, stop=kb["first"])
                if not kb["first"]:
                    ins = e.matmul(pst[:, 2 + l, :], lhsT=ones[:, :], rhs=Sb[j % 2][:, :], start=False, stop=True)
                return ins
            ex = [smark[i - 1]] if not kb["first"] else ([smark[i - 1]] if i > 0 else [])
            L.step("tensor", wmm, extra=ex)
            wmark[i] = L.mark()
            L.step("scalar", lambda e: e.activation(out=ew[l][:, :], in_=pst[:, 2 + l, :], func=AF.Exp, scale=-1.0))
            L.step("vector", lambda e: e.tensor_tensor(out=P[l][:, :], in0=e_[l][:, :], in1=ew[l][:, :], op=ALU.mult))
            if kb["m"] is not None:
                L.step("vector", lambda e: e.tensor_tensor(out=P[l][:, :], in0=P[l][:, :], in1=mk[:, kb["m"], :], op=ALU.mult))

            def pv(e):
                e.matmul(pst[:, 4 + 2 * par, :], lhsT=v[:, r * NBL + ip, 0:128], rhs=P[l][:, :], start=kb["first"], stop=kb["last"])
                return e.matmul(pst[:, 5 + 2 * par, :], lhsT=v[:, r * NBL + ip, 128:256], rhs=P[l][:, :], start=kb["first"], stop=kb["last"])
            extra = []
            if kb["first"] and g >= 2:
                extra.append(fin_marks[g - 2])
            L.step("tensor", pv, extra=extra)
            if kb["last"]:
                fin.step("vector", lambda e: [e.tensor_copy(out=ob[0][:, :], in_=pst[:, 4 + 2 * par, :]),
                                              e.tensor_copy(out=ob[1][:, :], in_=pst[:, 5 + 2 * par, :])], n=2, extra=[L.mark()])
                fin_marks[g] = fin.mark()
                fin.step("sync", lambda e: [e.dma_start(out=oT[h * 256 + dvh * 128:h * 256 + (dvh + 1) * 128, g * 512:(g + 1) * 512], in_=ob[dvh][:, :])
                                            for dvh in range(2)], n=2)

        for i in range(len(kbs)):
            add_A(i)
            if i >= 1:
                add_B(i - 1)
            add_S(i)
        add_B(len(kbs) - 1)
        sch.emit(block, [ld, fin, SL] + lanes)


def allgather_stage(nc, pairs):
    with contextlib.ExitStack() as es:
        sems = es.enter_context(stage_sems(nc, [f"cc{i}" for i in range(len(pairs))]))
        block = es.enter_context(nc.Block())

        @block.gpsimd
        def _(g):
            for (src, dst), sm in zip(pairs, sems):
                ins = g.collective_compute("AllGather", ALU.bypass, replica_groups=[list(range(NCORES))],
                                           ins=[src.ap().opt()], outs=[dst.ap().opt()])
                ins.then_inc(sm.h)
                g.wait_ge(sm.h, 1)


def wprep_stage(nc, items):
    with contextlib.ExitStack() as es:
        (cp,) = es.enter_context(stage_sems(nc, ["wcp"]))
        block = es.enter_context(nc.Block())

        @block.gpsimd
        def _(g):
            for ext, shard, _full in items:
                rows = ext.shape[0]
                for a, z in _chunks(rows, 128):
                    cp.inc(g.dma_start(out=shard.ap()[a:z, :], in_=ext[a:z, :]), 16)
            cp.wait_all(g)
    allgather_stage(nc, [(shard, full) for _e, shard, full in items])


def build_program(depth=DEPTH, debug=False):
    nc = bass.Bass("TRN2", target_bir_lowering=False)
    NLA, NLB = (depth + 1) // 2, depth // 2
    dumps = {}

    def dump(name, src, rows, dt):
        if not debug:
            return
        dst = nc.dram_tensor("dbg_" + name, [rows, 512], dt, kind="ExternalOutput").ap()
        with contextlib.ExitStack() as es:
            (sm,) = es.enter_context(stage_sems(nc, ["dbg"]))
            block = es.enter_context(nc.Block())

            @block.sync
            def _(e):
                for a, z in _chunks(rows, 1024):
                    sm.inc(e.dma_start(out=dst[a:z, :], in_=src[a:z, 0:512]), 16)
                sm.wait_all(e)

    def ext(name, shape, dt=F32):
        return nc.dram_tensor(name, list(shape), dt, kind="ExternalInput").ap()

    def internal(name, shape, dt):
        return nc.dram_tensor(name, list(shape), dt)

    xT = ext("xT", [D, T])
    pT = ext("pT", [depth, 256, T])
    posrep = ext("posrep", [64, T], I32)
    rope_c = ext("rope_c", [64, 2])
    g_mix = ext("g_mix", [depth, 128, 32])
    g_mlp = ext("g_mlp", [depth, 128, 32])
    g_ple = ext("g_ple", [depth, 128, 32])
    g_qn = ext("g_qn", [NLA, 128, 8])
    g_kvn = ext("g_kvn", [NLA, 128, 4])
    qk_gains = ext("qk_gains", [NLA, 128, 6])
    mask_mla = ext("mask_mla", [128, 32, 512], BF16)
    mask_sb = ext("mask_sb", [128, 32, 512], BF16)
    lincl = ext("lincl", [128, 128], BF16)
    outT = nc.dram_tensor("outT", [D, T], F32, kind="ExternalOutput").ap()

    wspec = {
        "w_in": (NLA, D, 1664), "w_uq": (NLA, 1024, 2048), "w_ukv": (NLA, 512, 2048), "w_o": (NLA, 1024, D),
        "w_qkv": (NLB, D, 3072), "sbw_o": (NLB, 1024, D),
        "w_up": (depth, D, DFF), "w_down": (depth, DFF, D), "w_proj": (depth, 256, D), "w_gate": (depth, D, D),
    }
    wspec = {k: v for k, v in wspec.items() if v[0] > 0}
    wext, wshard, wfull = {}, {}, {}
    for nm, (nl, K, N) in wspec.items():
        wext[nm] = ext(nm + "_s", [nl, K // NCORES, N])
        wshard[nm] = [internal(f"{nm}_sh{l}", [K // NCORES, N], BF16) for l in range(nl)]
        wfull[nm] = [internal(f"{nm}_f{l}", [K, N], BF16) for l in range(nl)]

    hT = internal("hT", [D, T], F32).ap()
    xnT = internal("xnT", [D, T], BF16).ap()
    HT = internal("HT", [DFF, T], BF16).ap()
    pprojT = internal("pprojT", [D, T], F32).ap()
    projT = internal("projT", [1664, T], F32).ap()
    cqT = internal("cqT", [1024, T], BF16).ap()
    ckvT = internal("ckvT", [512, T], BF16).ap()
    qT = internal("qT", [2048, T], F32).ap()
    knT = internal("knT", [1024, T], F32).ap()
    cos2T = internal("cos2T", [64, T], F32).ap()
    sin2sT = internal("sin2sT", [64, T], F32).ap()
    QTn = internal("QTn", [1024, T], BF16).ap()
    QTr = internal("QTr", [512, T], BF16).ap()
    KTn = internal("KTn", [1024, T], BF16)
    KTr = internal("KTr", [512, T], BF16)
    Vl = internal("Vl", [T, 1024], BF16)
    KTn_g = internal("KTn_g", [NCORES * 1024, T], BF16)
    KTr_g = internal("KTr_g", [NCORES * 512, T], BF16)
    V_g = internal("V_g", [NCORES * T, 1024], BF16)
    oT = internal("oT", [1024, T], BF16).ap()

    rope_stage(nc, posrep, rope_c, cos2T, sin2sT, T)

    def prep_layer(l):
        j = l // 2
        items = []
        names = ["w_in", "w_uq", "w_ukv", "w_o"] if l % 2 == 0 else ["w_qkv", "sbw_o"]
        for nm in names:
            items.append((wext[nm][j], wshard[nm][j], wfull[nm][j]))
        for nm in ["w_up", "w_down", "w_proj", "w_gate"]:
            items.append((wext[nm][l], wshard[nm][l], wfull[nm][l]))
        wprep_stage(nc, items)

    for l in range(depth):
        prep_layer(l)

    h_cur = xT
    for l in range(depth):
        j = l // 2
        last = l == depth - 1
        norm_stage(nc, h_cur, g_mix[l], xnT, D, T)
        if l % 2 == 0:
            gemm_stage(nc, xnT, wfull["w_in"][j].ap(), projT, D, 1664, T, mode="FM", epi="copy")
            norm_stage(nc, projT[0:1024, :], g_qn[j], cqT, 1024, T)
            norm_stage(nc, projT[1024:1536, :], g_kvn[j], ckvT, 512, T)
            gemm_stage(nc, cqT, wfull["w_uq"][j].ap(), qT, 1024, 2048, T, mode="FM", epi="copy")
            gemm_stage(nc, ckvT, wfull["w_ukv"][j].ap()[:, 0:1024], knT, 512, 1024, T, mode="FM", epi="copy")
            gemm_stage(nc, ckvT, wfull["w_ukv"][j].ap()[:, 1024:2048], Vl.ap(), 512, 1024, T, mode="TM", epi="copy")
            qkprep_stage(nc, qT, knT, projT, qk_gains[j], cos2T, sin2sT, QTn, QTr, KTn.ap(), KTr.ap(), T)
            if l == 0:
                dump("xn", xnT, D, BF16); dump("proj", projT, 1664, F32); dump("qT", qT, 2048, F32)
                dump("QTn", QTn, 1024, BF16); dump("QTr", QTr, 512, BF16); dump("KTn", KTn.ap(), 1024, BF16); dump("KTr", KTr.ap(), 512, BF16)
            allgather_stage(nc, [(KTn, KTn_g), (KTr, KTr_g), (Vl, V_g)])
            for h in range(8):
                mla_attn_head(nc, h, QTn, QTr, KTn_g.ap(), KTr_g.ap(), V_g.ap(), mask_mla, oT, T)
            if l == 0:
                dump("oT", oT, 1024, BF16)
            gemm_stage(nc, oT, wfull["w_o"][j].ap(), hT, 1024, D, T, mode="FM", epi="resid", aux=[h_cur])
            if l == 0:
                dump("hmix", hT, D, F32)
        else:
            wq = wfull["w_qkv"][j].ap()
            gemm_stage(nc, xnT, wq[:, 0:1024], QTn, D, 1024, T, mode="FM", epi="copy")
            gemm_stage(nc, xnT, wq[:, 1024:2048], KTn.ap(), D, 1024, T, mode="FM", epi="copy")
            gemm_stage(nc, xnT, wq[:, 2048:3072], Vl.ap(), D, 1024, T, mode="TM", epi="copy")
            allgather_stage(nc, [(KTn, KTn_g), (Vl, V_g)])
            for h in range(4):
                sb_attn_head(nc, h, QTn, KTn_g.ap(), V_g.ap(), mask_sb, lincl, oT, T)
            if l == 1:
                dump("sbq", QTn, 1024, BF16); dump("sboT", oT, 1024, BF16)
            gemm_stage(nc, oT, wfull["sbw_o"][j].ap(), hT, 1024, D, T, mode="FM", epi="resid", aux=[h_cur])
            if l == 1:
                dump("hmix1", hT, D, F32)
        h_cur = hT
        norm_stage(nc, hT, g_mlp[l], xnT, D, T)
        gemm_stage(nc, xnT, wfull["w_up"][l].ap(), HT, D, DFF, T, mode="FM", epi="relu2")
        gemm_stage(nc, HT, wfull["w_down"][l].ap(), hT, DFF, D, T, mode="FM", epi="resid", aux=[hT])
        if l == 0:
            dump("hmlp", hT, D, F32)
        norm_stage(nc, hT, g_ple[l], xnT, D, T)
        gemm_stage(nc, pT[l], wfull["w_proj"][l].ap(), pprojT, 256, D, T, mode="FM", epi="copy")
        gemm_stage(nc, xnT, wfull["w_gate"][l].ap(), outT if last else hT, D, D, T, mode="FM", epi="ple", aux=[hT, pprojT])
    return nc


_DEBUG = {}


def _tok_index(c):
    return np.concatenate([np.arange((NCORES * i + c) * 128, (NCORES * i + c + 1) * 128) for i in range(NBLK)])


def _g2d(g):
    R = g.shape[-1]
    return np.ascontiguousarray(g.reshape(g.shape[:-1] + (R // 128, 128)).swapaxes(-1, -2))


def _make_mask(c, kind):
    M = np.zeros((128, 32, 512), np.float32)
    s = np.arange(128)[:, None]
    tq = np.arange(128)[None, :]
    for ipr in range(4):
        for r in range(NCORES):
            for ir in range(4):
                kb = NCORES * ipr + r
                qb = NCORES * ir + c
                if kb < qb:
                    blk = 1.0
                elif kb > qb:
                    blk = 0.0
                else:
                    blk = ((s // 64) <= (tq // 64)) if kind == "mla" else (s < tq)
                M[:, ipr * NCORES + r, ir * 128:(ir + 1) * 128] = blk
    return M.astype(ml_dtypes.bfloat16)


def _shard_rows(w, c):
    K = w.shape[-2]
    ks = K // NCORES
    return np.ascontiguousarray(w[..., c * ks:(c + 1) * ks, :])


def kernel(x, p, positions, norm_mix, norm_mlp, norm_ple,
           mla_w_in, mla_q_norm, mla_kv_norm, mla_w_uq, mla_w_ukv,
           mla_q_gain, mla_k_gain, mla_w_o,
           sb_w_qkv, sb_w_o, mlp_w_up, mlp_w_down, ple_w_proj, ple_w_gate):
    f = lambda a: np.asarray(a)
    x, p, positions = f(x), f(p), f(positions)
    norm_mix, norm_mlp, norm_ple = f(norm_mix), f(norm_mlp), f(norm_ple)
    mla_w_in, mla_q_norm, mla_kv_norm, mla_w_uq, mla_w_ukv = f(mla_w_in), f(mla_q_norm), f(mla_kv_norm), f(mla_w_uq), f(mla_w_ukv)
    mla_q_gain, mla_k_gain, mla_w_o = f(mla_q_gain), f(mla_k_gain), f(mla_w_o)
    sb_w_qkv, sb_w_o, mlp_w_up, mlp_w_down, ple_w_proj, ple_w_gate = f(sb_w_qkv), f(sb_w_o), f(mlp_w_up), f(mlp_w_down), f(ple_w_proj), f(ple_w_gate)

    w_in_ext = np.concatenate([mla_w_in, mla_w_in[:, :, 1568:1600], mla_w_in[:, :, 1536:1568]], axis=2)
    uq = mla_w_uq.reshape(2, 1024, 8, 192)
    w_uq_p = np.concatenate([uq[..., :128].reshape(2, 1024, 1024), uq[..., 128:].reshape(2, 1024, 512),
                             np.concatenate([uq[..., 160:], uq[..., 128:160]], axis=-1).reshape(2, 1024, 512)], axis=2)
    ukv = mla_w_ukv.reshape(2, 512, 8, 256)
    w_ukv_p = np.concatenate([ukv[..., :128].reshape(2, 512, 1024), ukv[..., 128:].reshape(2, 512, 1024)], axis=2)
    qk_g = np.zeros((2, 128, 6), np.float32)
    for j in range(2):
        for o, g in ((0, mla_q_gain[j]), (3, mla_k_gain[j])):
            qk_g[j, :, o] = g[:128]
            qk_g[j, :64, o + 1] = g[128:]
            qk_g[j, :64, o + 2] = np.concatenate([g[160:], g[128:160]])
    inv_freq = (np.float32(10000.0) ** (-(np.arange(0, 64, 2, dtype=np.float32)) / np.float32(64))).astype(np.float32)
    rope_c = np.zeros((64, 2), np.float32)
    rope_c[:, 0] = np.tile(inv_freq, 2)
    rope_c[:32, 1] = -1.0
    rope_c[32:, 1] = 1.0
    lincl = (np.arange(128)[:, None] >= np.arange(128)[None, :]).astype(np.float32).astype(ml_dtypes.bfloat16)
    common = {
        "rope_c": rope_c, "lincl": lincl,
        "g_mix": _g2d(norm_mix), "g_mlp": _g2d(norm_mlp), "g_ple": _g2d(norm_ple),
        "g_qn": _g2d(mla_q_norm), "g_kvn": _g2d(mla_kv_norm), "qk_gains": qk_g,
    }
    weights = {"w_in": w_in_ext, "w_uq": w_uq_p, "w_ukv": w_ukv_p, "w_o": mla_w_o, "w_qkv": sb_w_qkv, "sbw_o": sb_w_o,
               "w_up": mlp_w_up, "w_down": mlp_w_down, "w_proj": ple_w_proj, "w_gate": ple_w_gate}
    xb = x[0].reshape(NBLK, NCORES, 128, D)
    pb = p[:, 0].reshape(DEPTH, NBLK, NCORES, 128, 256)
    posb = positions[0].reshape(NBLK, NCORES, 128)
    in_maps = []
    for c in range(NCORES):
        m = dict(common)
        m["xT"] = np.ascontiguousarray(xb[:, c].reshape(T, D).T)
        m["pT"] = np.ascontiguousarray(pb[:, :, c].reshape(DEPTH, T, 256).transpose(0, 2, 1))
        m["posrep"] = np.ascontiguousarray(np.broadcast_to(posb[:, c].reshape(1, T), (64, T))).astype(np.int32)
        m["mask_mla"] = _make_mask(c, "mla")
        m["mask_sb"] = _make_mask(c, "sb")
        for nm, w in weights.items():
            m[nm + "_s"] = _shard_rows(w, c)
        in_maps.append(m)

    if _DEBUG.get("depth"):
        dd = _DEBUG["depth"]
        for m in in_maps:
            for k in list(m.keys()):
                if k in ("pT", "g_mix", "g_mlp", "g_ple", "w_up_s", "w_down_s", "w_proj_s", "w_gate_s"):
                    m[k] = m[k][:dd]
                elif k in ("g_qn", "g_kvn", "qk_gains", "w_in_s", "w_uq_s", "w_ukv_s", "w_o_s"):
                    m[k] = m[k][:(dd + 1) // 2]
                elif k in ("w_qkv_s", "sbw_o_s"):
                    if dd // 2 == 0:
                        del m[k]
                    else:
                        m[k] = m[k][:dd // 2]
        nc = build_program(depth=dd, debug=True)
        res = run_bass_kernel_spmd(nc, in_maps, core_ids=list(range(NCORES)))
        _DEBUG["res"] = res
    else:
        nc = build_program()
        res = run_bass_kernel_spmd(nc, in_maps, core_ids=list(range(NCORES)))
    out = np.empty((1, SEQ, D), np.float32)
    ob = out[0].reshape(NBLK, NCORES, 128, D)
    for c in range(NCORES):
        ob[:, c] = res.results[c]["outT"].T.reshape(NBLK, 128, D)
    return out
```
